# Optimizing a Trainium2 kernel written in Bass

```python
import jax, jax.numpy as jnp
from jax import lax
import numpy as np

D_MODEL = 1024
BATCH = 4
SEQ = 4096
DEPTH = 4
DEC_BATCH = 8
DEC_SEQ = 8192
PAST_LEN = 128

N_META = 16
GRID_W = 64
HEAD_DIM = 64
ATTN_WIDTH = 512
N_Q_HEADS = ATTN_WIDTH // HEAD_DIM
N_KV_HEADS = 2
KV_GROUP = N_Q_HEADS // N_KV_HEADS
KV_WIDTH = N_KV_HEADS * HEAD_DIM
CONV_WIDTH = D_MODEL - ATTN_WIDTH
CONV_GROUPS = CONV_WIDTH // HEAD_DIM
D_FF = 2816
Q_BLOCK = 128
ROPE_THETA = 10000.0
ROPE_PAIRS_AXIS = HEAD_DIM // 4
EPS = 1e-6
IN_WIDTH = ATTN_WIDTH + 2 * KV_WIDTH + 3 * CONV_WIDTH
SPLIT_POINTS = (ATTN_WIDTH,
                ATTN_WIDTH + KV_WIDTH,
                ATTN_WIDTH + 2 * KV_WIDTH,
                ATTN_WIDTH + 2 * KV_WIDTH + CONV_WIDTH,
                ATTN_WIDTH + 2 * KV_WIDTH + 2 * CONV_WIDTH)

kernel_name = "hymba_conv_axial_gqa_macaron_encoder"


def rms_norm(x, g):
    xf = x.astype(jnp.float32)
    y = xf * lax.rsqrt(jnp.mean(xf * xf, axis=-1, keepdims=True) + EPS)
    return (y * g.astype(jnp.float32)).astype(x.dtype)


def group_rms_norm(x, g, n_groups):
    b, l, w = x.shape
    xf = x.astype(jnp.float32).reshape(b, l, n_groups, w // n_groups)
    y = xf * lax.rsqrt(jnp.mean(xf * xf, axis=-1, keepdims=True) + EPS)
    return (y.reshape(b, l, w) * g.astype(jnp.float32)).astype(x.dtype)


def swiglu(x, w_gate, w_up, w_down):
    return (jax.nn.silu(x @ w_gate) * (x @ w_up)) @ w_down


def rope_tables(n_tokens):
    rows = n_tokens // GRID_W
    row = jnp.repeat(jnp.arange(rows, dtype=jnp.float32), GRID_W)
    col = jnp.tile(jnp.arange(GRID_W, dtype=jnp.float32), rows)
    row = jnp.concatenate([jnp.zeros((N_META,), jnp.float32), row])
    col = jnp.concatenate([jnp.zeros((N_META,), jnp.float32), col])
    freqs = ROPE_THETA ** (-jnp.arange(ROPE_PAIRS_AXIS, dtype=jnp.float32) / ROPE_PAIRS_AXIS)
    ang = jnp.concatenate([row[:, None] * freqs, col[:, None] * freqs], axis=-1)
    return jnp.cos(ang), jnp.sin(ang)


def apply_rope(x, cos, sin):
    b, l, h, d = x.shape
    xf = x.astype(jnp.float32).reshape(b, l, h, d // 2, 2)
    x0, x1 = xf[..., 0], xf[..., 1]
    c = cos[None, :, None, :]
    s = sin[None, :, None, :]
    out = jnp.stack([x0 * c - x1 * s, x0 * s + x1 * c], axis=-1)
    return out.reshape(b, l, h, d).astype(x.dtype)


def attend(qb, k, v):
    s = jnp.einsum("bqhgd,bkhd->bhgqk", qb, k, preferred_element_type=jnp.float32)
    p = jax.nn.softmax(s * (HEAD_DIM ** -0.5), axis=-1).astype(v.dtype)
    return jnp.einsum("bhgqk,bkhd->bqhgd", p, v)


def attention_group(a_q, a_k, a_v, q_gain, k_gain, cos, sin):
    b, l, _ = a_q.shape
    q = rms_norm(a_q.reshape(b, l, N_Q_HEADS, HEAD_DIM), q_gain)
    k = rms_norm(a_k.reshape(b, l, N_KV_HEADS, HEAD_DIM), k_gain)
    v = a_v.reshape(b, l, N_KV_HEADS, HEAD_DIM)
    q = apply_rope(q, cos, sin).reshape(b, l, N_KV_HEADS, KV_GROUP, HEAD_DIM)
    k = apply_rope(k, cos, sin)
    o_meta = attend(q[:, :N_META], k, v).reshape(b, N_META, ATTN_WIDTH)
    n = l - N_META
    n_blk = n // Q_BLOCK
    q_blocks = q[:, N_META:].reshape(b, n_blk, Q_BLOCK, N_KV_HEADS, KV_GROUP, HEAD_DIM)
    q_blocks = jnp.moveaxis(q_blocks, 1, 0)
    o = lax.map(lambda qb: attend(qb, k, v), q_blocks)
    o = jnp.moveaxis(o, 0, 1).reshape(b, n, ATTN_WIDTH)
    return jnp.concatenate([o_meta, o], axis=1)


def conv_group(c_b, c_c, c_h, w, bias):
    u = c_c * c_h
    up = jnp.pad(u, ((0, 0), (1, 1), (0, 0)))
    y = up[:, :-2] * w[0] + up[:, 1:-1] * w[1] + up[:, 2:] * w[2] + bias
    return c_b * y


def trunk(x, meta_tokens, ffn1_norm, ffn1_w_gate, ffn1_w_up, ffn1_w_down, mix_norm, w_in,
          q_norm, k_norm, conv_w, conv_b, attn_out_norm, conv_out_norm, w_out,
          ffn2_norm, ffn2_w_gate, ffn2_w_up, ffn2_w_down, final_norm):
    b, n, d = x.shape
    meta = jnp.broadcast_to(meta_tokens[None].astype(x.dtype), (b, N_META, d))
    h = jnp.concatenate([meta, x], axis=1)
    cos, sin = rope_tables(n)
    for l in range(DEPTH):
        h = h + 0.5 * swiglu(rms_norm(h, ffn1_norm[l]), ffn1_w_gate[l], ffn1_w_up[l], ffn1_w_down[l])
        u = rms_norm(h, mix_norm[l]) @ w_in[l]
        a_q, a_k, a_v, c_b, c_c, c_h = jnp.split(u, SPLIT_POINTS, axis=-1)
        y_att = attention_group(a_q, a_k, a_v, q_norm[l], k_norm[l], cos, sin)
        y_conv = conv_group(c_b, c_c, c_h, conv_w[l], conv_b[l])
        y_att = group_rms_norm(y_att, attn_out_norm[l], N_Q_HEADS)
        y_conv = group_rms_norm(y_conv, conv_out_norm[l], CONV_GROUPS)
        h = h + jnp.concatenate([y_att, y_conv], axis=-1) @ w_out[l]
        h = h + 0.5 * swiglu(rms_norm(h, ffn2_norm[l]), ffn2_w_gate[l], ffn2_w_up[l], ffn2_w_down[l])
    return rms_norm(h[:, N_META:], final_norm)


def setup_inputs(seed: int = 0) -> dict:
    key = jax.random.key(seed)
    ks = jax.random.split(key, 24)
    f32 = jnp.float32

    def nrm(k, shape, scale):
        return jax.random.normal(k, shape, f32) * scale

    def gain(k, shape):
        return 1.0 + 0.01 * jax.random.normal(k, shape, f32)

    return {
        "x_prompt": nrm(ks[0], (BATCH, SEQ, D_MODEL), 1.0),
        "x_sample": nrm(ks[1], (DEC_BATCH, DEC_SEQ, D_MODEL), 1.0),
        "meta_tokens": nrm(ks[2], (N_META, D_MODEL), 1.0),
        "ffn1_norm": gain(ks[3], (DEPTH, D_MODEL)),
        "ffn1_w_gate": nrm(ks[4], (DEPTH, D_MODEL, D_FF), D_MODEL ** -0.5),
        "ffn1_w_up": nrm(ks[5], (DEPTH, D_MODEL, D_FF), D_MODEL ** -0.5),
        "ffn1_w_down": nrm(ks[6], (DEPTH, D_FF, D_MODEL), D_FF ** -0.5),
        "mix_norm": gain(ks[7], (DEPTH, D_MODEL)),
        "w_in": nrm(ks[8], (DEPTH, D_MODEL, IN_WIDTH), D_MODEL ** -0.5),
        "q_norm": gain(ks[9], (DEPTH, HEAD_DIM)),
        "k_norm": gain(ks[10], (DEPTH, HEAD_DIM)),
        "conv_w": nrm(ks[11], (DEPTH, 3, CONV_WIDTH), 3 ** -0.5),
        "conv_b": nrm(ks[12], (DEPTH, CONV_WIDTH), 0.01),
        "attn_out_norm": gain(ks[13], (DEPTH, ATTN_WIDTH)),
        "conv_out_norm": gain(ks[14], (DEPTH, CONV_WIDTH)),
        "w_out": nrm(ks[15], (DEPTH, D_MODEL, D_MODEL), D_MODEL ** -0.5),
        "ffn2_norm": gain(ks[16], (DEPTH, D_MODEL)),
        "ffn2_w_gate": nrm(ks[17], (DEPTH, D_MODEL, D_FF), D_MODEL ** -0.5),
        "ffn2_w_up": nrm(ks[18], (DEPTH, D_MODEL, D_FF), D_MODEL ** -0.5),
        "ffn2_w_down": nrm(ks[19], (DEPTH, D_FF, D_MODEL), D_FF ** -0.5),
        "final_norm": gain(ks[20], (D_MODEL,)),
    }


def reference(x_prompt, x_sample, meta_tokens, ffn1_norm, ffn1_w_gate, ffn1_w_up, ffn1_w_down,
              mix_norm, w_in, q_norm, k_norm, conv_w, conv_b, attn_out_norm, conv_out_norm, w_out,
              ffn2_norm, ffn2_w_gate, ffn2_w_up, ffn2_w_down, final_norm):
    y_prompt = trunk(x_prompt, meta_tokens, ffn1_norm, ffn1_w_gate, ffn1_w_up, ffn1_w_down,
                     mix_norm, w_in, q_norm, k_norm, conv_w, conv_b, attn_out_norm, conv_out_norm,
                     w_out, ffn2_norm, ffn2_w_gate, ffn2_w_up, ffn2_w_down, final_norm)
    y_sample = trunk(x_sample, meta_tokens, ffn1_norm, ffn1_w_gate, ffn1_w_up, ffn1_w_down,
                     mix_norm, w_in, q_norm, k_norm, conv_w, conv_b, attn_out_norm, conv_out_norm,
                     w_out, ffn2_norm, ffn2_w_gate, ffn2_w_up, ffn2_w_down, final_norm)
    return (y_prompt, y_sample)
```

```python
import numpy as np
import concourse.bass as bass
import concourse.mybir as mybir
from concourse.bass_utils import run_bass_kernel_spmd

F32 = mybir.dt.float32
BF16 = mybir.dt.bfloat16
AF = mybir.ActivationFunctionType
ALU = mybir.AluOpType

D = 1024
KC = 8
DFF = 2816
NJ = 22
NMETA = 16
TT = 512
EPS = 1e-6
NCORES = 8
SLOTW = 2816
NSLOT = 6
NDUMMY = 0


class Op:
    __slots__ = ("eng", "fn", "deps", "flag", "count", "dsem", "dval", "idx", "pos")


class Sched:
    COMPUTE = ("pe", "act", "dve", "pool")
    SAME_ENG_WINDOW = 6

    def __init__(self, nc):
        self.nc = nc
        self.eobj = {"pe": nc.tensor, "act": nc.scalar, "dve": nc.vector, "pool": nc.gpsimd, "sp": nc.sync}
        self.esem = {e: nc.alloc_semaphore("S_" + e) for e in self.COMPUTE}
        self.streams = {e: [] for e in self.eobj}
        self.last_writer = {}
        self.readers = {}
        self.dsems = {}
        self.dvals = {}
        self.dlast = {}
        self.nops = 0

    def op(self, eng, meth, reads=(), writes=(), dsem=None, a=(), **kw):
        o = Op()
        o.eng = eng; o.fn = (meth, a, kw); o.flag = False; o.count = 0; o.dsem = None; o.dval = 0
        o.idx = self.nops; self.nops += 1
        o.pos = len(self.streams[eng])
        deps = {}
        lw = self.last_writer; rd = self.readers
        for k in reads:
            y = lw.get(k)
            if y is not None:
                deps[y.idx] = y
        for k in writes:
            y = lw.get(k)
            if y is not None:
                deps[y.idx] = y
            r = rd.get(k)
            if r:
                for y in r.values():
                    deps[y.idx] = y
        if dsem is not None:
            if dsem not in self.dsems:
                self.dsems[dsem] = self.nc.alloc_semaphore("D_" + str(len(self.dsems)))
                self.dvals[dsem] = 0
            prev = self.dlast.get(dsem)
            if prev is not None:
                deps[prev.idx] = prev
            self.dvals[dsem] += 16
            o.dsem = self.dsems[dsem]; o.dval = self.dvals[dsem]
            self.dlast[dsem] = o
        dl = []
        for y in deps.values():
            if y.dsem is not None:
                dl.append(y)
            elif y.eng == eng and o.dsem is None and (eng == "pe" or o.pos - y.pos > self.SAME_ENG_WINDOW):
                continue
            else:
                y.flag = True
                dl.append(y)
        o.deps = dl
        for k in reads:
            r = rd.get(k)
            if r is None:
                rd[k] = {eng: o}
            else:
                r[eng] = o
        for k in writes:
            lw[k] = o
            rd[k] = {}
        self.streams[eng].append(o)
        return o

    def emit(self, final_sems=()):
        nc = self.nc
        for e in self.COMPUTE:
            c = 0
            for o in self.streams[e]:
                if o.dsem is None and o.flag:
                    c += 1
                    o.count = c
        esem = self.esem
        with nc.Block() as block:
            def run(e, eng):
                waited = {}
                for o in self.streams[e]:
                    need = {}
                    for y in o.deps:
                        if y.dsem is not None:
                            s, v = y.dsem, y.dval
                        else:
                            s, v = esem[y.eng], y.count
                        if need.get(s, (None, 0))[1] < v:
                            need[s] = (s, v)
                    for s, v in need.values():
                        if waited.get(s, 0) < v:
                            eng.wait_ge(s, v)
                            waited[s] = v
                    meth, a, kw = o.fn
                    ins = getattr(eng, meth)(*a, **kw)
                    if o.dsem is not None:
                        ins.then_inc(o.dsem, 16)
                    elif o.flag:
                        ins.then_inc(esem[e], 1)
                if e == "sp":
                    for k in final_sems:
                        eng.wait_ge(self.dsems[k], self.dvals[k])

            @block.sync
            def _(eng):
                run("sp", eng)

            @block.tensor
            def _(eng):
                run("pe", eng)

            @block.scalar
            def _(eng):
                run("act", eng)

            @block.vector
            def _(eng):
                run("dve", eng)

            @block.gpsimd
            def _(eng):
                run("pool", eng)


def vec_layout(depth):
    off = {}
    c = 0
    for name, w in (("g1", depth * 8), ("gm", depth * 8), ("g2", depth * 8), ("gf", 8),
                    ("gq", depth), ("gk", depth), ("cw", depth * 12), ("cb", depth * 4),
                    ("ga", depth * 8), ("gc", depth * 4), ("eps", 1)):
        off[name] = c
        c += w
    return off, c


CST_RT, CST_ONESD, CST_BD64, CST_G65, CST_ONES, CST_N = 0, 128, 256, 384, 448, 576


def build(n_real=(8192, 4096), depth=4, debug=False):
    nseq = len(n_real)
    L = [n + NMETA for n in n_real]
    nmax = max(n_real)
    voff, NV = vec_layout(depth)
    nc = bass.Bass("TRN2", target_bir_lowering=False)

    def din(name, shape, dt=F32):
        return nc.dram_tensor(name, list(shape), dt, kind="ExternalInput").ap()

    def dscr(name, shape, dt):
        return nc.dram_tensor(name, list(shape), dt, kind="Internal").ap()

    xT = [din(f"xT{s}", [KC, 128, n_real[s]]) for s in range(nseq)]
    metaT = din("metaT", [KC, 128, NMETA])
    wf = {"gu0": din("gu0", [depth, NJ, 128, 2048]), "gu1": din("gu1", [depth, NJ, 128, 2048]),
          "dn0": din("dn0", [depth, 8, 128, SLOTW]), "dn1": din("dn1", [depth, 8, 128, SLOTW]),
          "win": din("win", [depth, 18, 128, 1024]),
          "woa": din("woa", [depth, 8, 64, 1024]), "woc": din("woc", [depth, 8, 128, 512])}
    wb = {k: dscr(k + "_b", v.shape, BF16) for k, v in wf.items()}
    vecs_d = din("vecs", [128, NV])
    gqk_d = din("gqk", [1, depth * 128])
    cst_d = din("cst", [128, CST_N])
    cos_d = din("cosT", [128, NMETA + nmax])
    sin_d = din("sinT", [128, NMETA + nmax])
    yT = [nc.dram_tensor(f"yT{s}", [KC, 128, n_real[s]], F32, kind="ExternalOutput").ap() for s in range(nseq)]
    if debug:
        hD = [nc.dram_tensor(f"hD{s}", [KC, 128, L[s]], F32, kind="ExternalOutput").ap() for s in range(nseq)]
    else:
        hD = [dscr(f"hD{s}", [KC, 128, L[s]], F32) for s in range(nseq)]
    qD = [dscr(f"qD{s}", [4, 128, L[s]], BF16) for s in range(nseq)]
    cbD = [dscr(f"cbD{s}", [4, 128, L[s]], BF16) for s in range(nseq)]
    uD = [dscr(f"uD{s}", [4, 128, L[s] + 2], BF16) for s in range(nseq)]

    sb = nc.alloc_sbuf_tensor
    NCH = [1 + n // 128 for n in n_real]
    KT = [sb(f"KT{s}", [128, L[s]], BF16) for s in range(nseq)]
    V = [sb(f"V{s}", [128, NCH[s], 130], BF16) for s in range(nseq)]
    h = sb("h", [128, KC, TT], F32)
    xn = sb("xn", [128, KC, TT], BF16)
    xnb = sb("xnb", [128, KC, TT], BF16)
    act = sb("act", [128, NJ, TT], BF16)
    wslot = [sb(f"w{i}", [128, SLOTW], BF16) for i in range(NSLOT)]
    qbuf = sb("qbuf", [128, 4, TT], BF16)
    cbbuf = sb("cbbuf", [128, 4, TT], BF16)
    ubuf = sb("ubuf", [128, 4, TT + 2], BF16)
    cs = sb("cs", [128, 2, TT], F32)
    NF = 8
    ft = [sb(f"ft{i}", [128, TT], F32) for i in range(NF)]
    NB = 4
    bt = [sb(f"bt{i}", [128, TT], BF16) for i in range(NB)]
    NP = 4
    pT = [sb(f"pT{i}", [128, 2, TT], BF16) for i in range(NP)]
    yatt = sb("yatt", [64, 8, TT], BF16)
    yconv = sb("yconv", [128, 4, TT], BF16)
    vecs = sb("vecs_sb", [128, NV], F32)
    cstf = sb("cstf", [128, CST_N], F32)
    cstb = sb("cstb", [128, CST_N], BF16)
    gqk = sb("gqk_sb", [1, depth * 128], F32)
    smax = sb("smax", [1, 4], F32)
    gneg = sb("gneg_sb", [1, depth * 128], F32)
    negb = sb("negb", [128, depth], F32)
    zero = sb("zero", [128, 8], BF16)
    psS = [nc.alloc_psum_tensor(f"psS{i}", [128, 2 * TT], F32) for i in range(2)]
    ps = [psS[i // 2][:, (i % 2) * TT:(i % 2 + 1) * TT] for i in range(4)] + \
         [nc.alloc_psum_tensor(f"ps{i}", [128, TT], F32)[:, :] for i in range(4, 8)]

    S = Sched(nc)
    op = S.op
    AX = mybir.AxisListType.X

    def mm(out, lhsT, rhs, start, stop, reads, writes):
        op("pe", "matmul", reads, writes, a=(out,), lhsT=lhsT, rhs=rhs, start=start, stop=stop)

    def actf(out, in_, func, reads, writes, **kw):
        op("act", "activation", reads, writes, out=out, in_=in_, func=func, **kw)

    def tt(eng, out, in0, in1, alu, reads, writes):
        op(eng, "tensor_tensor", reads, writes, out=out, in0=in0, in1=in1, op=alu)

    def stt(eng, out, in0, scalar, in1, op0, op1, reads, writes):
        op(eng, "scalar_tensor_tensor", reads, writes, out=out, in0=in0, scalar=scalar, in1=in1, op0=op0, op1=op1)

    def recip(ap, key):
        op("dve", "reciprocal", [key], [key], out=ap, in_=ap)

    def dma(out, in_, reads, writes, dsem, eng="sp", **kw):
        op(eng, "dma_start", reads, writes, dsem=dsem, out=out, in_=in_, **kw)

    def vcol(name, c, parts=128):
        return vecs[0:parts, voff[name] + c: voff[name] + c + 1]

    HK = [("h", kc) for kc in range(KC)]

    dma(vecs[:], vecs_d, [], ["vecs"], "c0")
    dma(cstf[:], cst_d, [], ["cstf"], "c1")
    dma(gqk[:], gqk_d, [], ["gqk"], "c2")
    op("dve", "tensor_copy", ["cstf"], ["cstb"], out=cstb[:], in_=cstf[:])
    op("dve", "memset", [], ["zero"], a=(zero[:], 0.0))
    for s in range(nseq):
        op("dve", "memset", [], [("Vones", s)], a=(V[s][:, :, 64:65], 1.0))
        op("dve", "memset", [], [("Vones", s)], a=(V[s][:, :, 129:130], 1.0))
        for side in (0, 1):
            col = 0 if side == 0 else L[s] + 1
            dma(uD[s][:, :, col:col + 1].rearrange("c p o -> p c o"),
                zero[:, 0:4].rearrange("p (c o) -> p c o", o=1),
                ["zero"], [("uDpad", s, side)], ("pad", s, side), allow_slow_non_contiguous=True)
    for l in range(depth):
        gl = gqk[:, l * 128:(l + 1) * 128]
        gn = gneg[:, l * 128:(l + 1) * 128]
        op("dve", "tensor_scalar", ["gqk"], ["gneg"], out=gn, in0=gl, scalar1=-1.0, scalar2=None, op0=ALU.mult)
        tt("dve", gl, gl, gn, ALU.max, ["gqk", "gneg"], ["gqk"])
        op("dve", "reduce_max", ["gqk"], ["smax"], out=smax[:, 0:1], in_=gqk[:, l * 128:l * 128 + 64], axis=AX)
        op("dve", "reduce_max", ["gqk"], ["smax"], out=smax[:, 1:2], in_=gqk[:, l * 128 + 64:l * 128 + 128], axis=AX)
        tt("dve", smax[:, 2:3], smax[:, 0:1], smax[:, 1:2], ALU.mult, ["smax"], ["smax"])
        op("dve", "tensor_scalar", ["smax"], ["smax"], out=smax[:, 3:4], in0=smax[:, 2:3], scalar1=-8.0,
           scalar2=None, op0=ALU.mult)
        mm(ps[7][:, 0:1], cstf[0:1, CST_ONES:CST_ONES + 128], smax[0:1, 3:4], True, True,
           ["smax", "cstf"], [("ps", 7)])
        actf(negb[:, l:l + 1], ps[7][:, 0:1], AF.Copy, [("ps", 7)], ["negb"])

    ncast = [0]
    PER = {"gu0": 2, "gu1": 2, "dn0": 2, "dn1": 2, "win": 6, "woa": 8, "woc": 8}

    def cast(name, l):
        src = wf[name][l]; dst = wb[name][l]
        nb = src.shape[0]
        per = PER[name]
        for b0 in range(0, nb, per):
            b1 = min(nb, b0 + per)
            k = ncast[0] % (3 if (l == 0 and name in ("gu0", "dn0", "win")) else 8); ncast[0] += 1
            dma(dst[b0:b1].rearrange("b p x -> (b p) x"), src[b0:b1].rearrange("b p x -> (b p) x"),
                [], [("wb", name, l, b) for b in range(b0, b1)], ("cast", k), eng="pool")

    for l in range(depth):
        for name in ("gu0", "dn0", "win", "woa", "woc", "gu1", "dn1"):
            cast(name, l)

    ring = [0]

    def wload(name, l, b, width, parts=128, col0=0, newslot=True):
        if newslot:
            ring[0] += 1
        s = ring[0] % NSLOT
        dma(wslot[s][0:parts, col0:col0 + width], wb[name][l, b], [("wb", name, l, b)], [("w", s, col0)],
            ("w", s, col0))
        return s

    rr = {"ft": 0, "bt": 0, "pT": 0}

    def nxt(kind, n):
        rr[kind] = (rr[kind] + 1) % n
        return rr[kind]

    def mean_sq_all(n):
        for kc in range(KC):
            b = nxt("bt", NB)
            actf(bt[b][:, :n], h[:, kc, :n], AF.Square, [("h", kc)], [("bt", b)])
            mm(ps[6][:, :n], cstb[:, CST_ONESD:CST_ONESD + 128], bt[b][:, :n], kc == 0, kc == KC - 1,
               [("bt", b), "cstb"], [("ps", 6)])

    def rstd_from_ps6(n, parts, with_eps):
        f = nxt("ft", NF)
        if with_eps:
            actf(ft[f][0:parts, :n], ps[6][0:parts, :n], AF.Ln, [("ps", 6), "vecs"], [("ft", f)],
                 bias=vcol("eps", 0, parts))
        else:
            actf(ft[f][0:parts, :n], ps[6][0:parts, :n], AF.Ln, [("ps", 6)], [("ft", f)])
        actf(ft[f][0:parts, :n], ft[f][0:parts, :n], AF.Exp, [("ft", f)], [("ft", f)], scale=-0.5)
        return f

    def rms_to_xn(n, gname, gbase, dst=None, dkey="xn"):
        dst = xn if dst is None else dst
        mean_sq_all(n)
        f = rstd_from_ps6(n, 128, True)
        for kc in range(KC):
            stt("dve", dst[:, kc, :n], h[:, kc, :n], vcol(gname, gbase + kc), ft[f][:, :n], ALU.mult, ALU.mult,
                [("h", kc), ("ft", f), "vecs"], [(dkey, kc)])

    def ffn(n, l, which, prenormed=False):
        if not prenormed:
            rms_to_xn(n, "g1" if which == 0 else "g2", l * 8)
        gu = "gu%d" % which; dn = "dn%d" % which
        for j in range(NJ):
            s = wload(gu, l, j, 2048)
            pg = j % 2; pu = 2 + j % 2
            for kc in range(KC):
                mm(ps[pg][:, :n], wslot[s][:, kc * 128:(kc + 1) * 128], xn[:, kc, :n], kc == 0, kc == KC - 1,
                   [("w", s, 0), ("xn", kc)], [("ps", pg)])
            for kc in range(KC):
                mm(ps[pu][:, :n], wslot[s][:, 1024 + kc * 128:1024 + (kc + 1) * 128], xn[:, kc, :n], kc == 0,
                   kc == KC - 1, [("w", s, 0), ("xn", kc)], [("ps", pu)])
            f = nxt("ft", NF)
            actf(ft[f][:, :n], ps[pg][:, :n], AF.Silu, [("ps", pg)], [("ft", f)])
            tt("dve", act[:, j, :n], ft[f][:, :n], ps[pu][:, :n], ALU.mult, [("ft", f), ("ps", pu)], [("act", j)])
        for i in range(KC):
            s = wload(dn, l, i, SLOTW)
            po = 4 + i % 2
            for j in range(NJ):
                mm(ps[po][:, :n], wslot[s][:, j * 128:(j + 1) * 128], act[:, j, :n], j == 0, j == NJ - 1,
                   [("w", s, 0), ("act", j)], [("ps", po)])
            stt("dve", h[:, i, :n], ps[po][:, :n], 0.5, h[:, i, :n], ALU.mult, ALU.add,
                [("ps", po), ("h", i)], [("h", i)])

    def group_rstd(src_ap, src_key, n, parts, lhsT_ap, out_parts, with_eps):
        b = nxt("bt", NB)
        actf(bt[b][0:parts, :n], src_ap, AF.Square, [src_key], [("bt", b)])
        mm(ps[6][0:out_parts, :n], lhsT_ap, bt[b][0:parts, :n], True, True, [("bt", b), "cstb"], [("ps", 6)])
        return rstd_from_ps6(n, out_parts, with_eps)

    def qk_epilogue(pq, n, l, is_q, dst_ap, dst_key):
        st = {}

        def stage1():
            f = group_rstd(ps[pq][:, :n], ("ps", pq), n, 128, cstb[:, CST_BD64:CST_BD64 + 128], 128, True)
            g = nxt("ft", NF)
            stt("dve", ft[g][:, :n], ps[pq][:, :n], vcol("gq" if is_q else "gk", l), ft[f][:, :n], ALU.mult,
                ALU.mult, [("ps", pq), ("ft", f), "vecs"], [("ft", g)])
            st["f"] = f; st["g"] = g

        def stage2():
            f = st["f"]; g = st["g"]
            mm(ps[7][:, :n], cstf[:, CST_RT:CST_RT + 128], ft[g][:, :n], True, True, [("ft", g), "cstf"], [("ps", 7)])
            tt("dve", ft[f][:, :n], ps[7][:, :n], cs[:, 1, :n], ALU.mult, [("ps", 7), "cs1"], [("ft", f)])
            tt("pool", ft[g][:, :n], ft[g][:, :n], cs[:, 0, :n], ALU.mult, [("ft", g), "cs0"], [("ft", g)])
            tt("dve", dst_ap, ft[f][:, :n], ft[g][:, :n], ALU.add, [("ft", f), ("ft", g)], [dst_key])

        return [stage1, stage2]

    def tiles_of(s):
        t = [(s, 0, NMETA, None)]
        for r0 in range(0, n_real[s], TT):
            t.append((s, NMETA + r0, min(TT, n_real[s] - r0), r0))
        return t

    all_tiles = []
    for s in range(nseq):
        all_tiles += tiles_of(s)

    def load_h(s, p0, n, r0, l):
        if l == 0:
            src = metaT if r0 is None else xT[s][:, :, r0:r0 + n]
            dma(h[:, :, :n], src.rearrange("c p t -> p c t"), [], HK, "h")
        else:
            dma(h[:, :, :n], hD[s][:, :, p0:p0 + n].rearrange("c p t -> p c t"), [("hD", s, p0)], HK, "h")

    def store_h(s, p0, n):
        dma(hD[s][:, :, p0:p0 + n].rearrange("c p t -> p c t"), h[:, :, :n], HK, [("hD", s, p0)], "h")

    WIN_ORDER = [0, 1, 2, 3, 4, 5, 6, 7, 8, 9, 10, 14, 11, 15, 12, 16, 13, 17]

    def pass_a(l, tile, nxt_tile=None, prenormed=False):
        s, p0, n, r0 = tile
        if not prenormed:
            load_h(s, p0, n, r0, l)
        dma(cs[:, 0, :n], cos_d[:, p0:p0 + n], [], ["cs0"], "cs0")
        dma(cs[:, 1, :n], sin_d[:, p0:p0 + n], [], ["cs1"], "cs1")
        ffn(n, l, 0, prenormed)
        store_h(s, p0, n)
        rms_to_xn(n, "gm", l * 8, dst=xnb, dkey="xnb")
        pending_cc = {}
        sl = None
        queue = []
        for idx, oc in enumerate(WIN_ORDER):
            half = idx % 2
            if half == 0:
                sl = wload("win", l, oc, 1024, col0=0)
                wload("win", l, WIN_ORDER[idx + 1], 1024, col0=1024, newslot=False)
            c0 = half * 1024
            wk = ("w", sl, c0)
            if oc == 5:
                for item in list(queue):
                    item.pop(0)()
                    if not item:
                        queue.remove(item)
                nblk = (n + 127) // 128
                for b in range(nblk):
                    tn = min(128, n - b * 128)
                    for kc in range(KC):
                        mm(ps[7][0:tn, b * 128:(b + 1) * 128], xnb[:, kc, b * 128:b * 128 + tn],
                           wslot[sl][:, c0 + kc * 128:c0 + (kc + 1) * 128], kc == 0, kc == KC - 1,
                           [wk, ("xnb", kc)], [("ps", 7)])
                ch0 = 0 if r0 is None else 1 + r0 // 128
                tn = min(128, n)
                for g in range(2):
                    src = ps[7][0:tn, 0:nblk * 128].rearrange("p (b c) -> p b c", c=128)[:, :, g * 64:(g + 1) * 64]
                    actf(V[s][0:tn, ch0:ch0 + nblk, g * 65:g * 65 + 64], src, AF.Copy, [("ps", 7)], [("V", s, p0)])
                continue
            if idx == 10 and nxt_tile is not None:
                s2, p02, n2, r02 = nxt_tile
                load_h(s2, p02, n2, r02, l)
                rms_to_xn(n2, "g1", l * 8)
            pq = idx % 4
            for kc in range(KC):
                mm(ps[pq][:, :n], wslot[sl][:, c0 + kc * 128:c0 + (kc + 1) * 128], xnb[:, kc, :n], kc == 0,
                   kc == KC - 1, [wk, ("xnb", kc)], [("ps", pq)])
            for item in list(queue):
                item.pop(0)()
                if not item:
                    queue.remove(item)
            if oc < 4:
                queue.append(qk_epilogue(pq, n, l, True, qbuf[:, oc, :n], ("qbuf", oc)))
            elif oc == 4:
                queue.append(qk_epilogue(pq, n, l, False, KT[s][:, p0:p0 + n], ("KT", s, p0)))
            elif oc < 10:
                cc = oc - 6
                actf(cbbuf[:, cc, :n], ps[pq][:, :n], AF.Copy, [("ps", pq)], [("cbbuf", cc)])
            elif oc < 14:
                f = nxt("ft", NF)
                pending_cc[oc - 10] = f
                actf(ft[f][:, :n], ps[pq][:, :n], AF.Copy, [("ps", pq)], [("ft", f)])
            else:
                cc = oc - 14
                f = pending_cc[cc]
                tt("dve", ubuf[:, cc, 1:n + 1], ft[f][:, :n], ps[pq][:, :n], ALU.mult,
                   [("ft", f), ("ps", pq)], [("ubuf", cc)])
        for item in queue:
            for fn in item:
                fn()
        dma(qD[s][:, :, p0:p0 + n].rearrange("c p t -> p c t"), qbuf[:, :, :n],
            [("qbuf", c) for c in range(4)], [("qD", s, p0)], "qbuf")
        dma(cbD[s][:, :, p0:p0 + n].rearrange("c p t -> p c t"), cbbuf[:, :, :n],
            [("cbbuf", c) for c in range(4)], [("cbD", s, p0)], "cbbuf")
        dma(uD[s][:, :, 1 + p0:1 + p0 + n].rearrange("c p t -> p c t"), ubuf[:, :, 1:n + 1],
            [("ubuf", c) for c in range(4)], [("uD", s, p0)], "ubuf")

    def pass_b(l, tile, last):
        s, p0, n, r0 = tile
        tl = tiles_of(s)
        load_h(s, p0, n, r0, 1)
        dma(qbuf[:, :, :n], qD[s][:, :, p0:p0 + n].rearrange("c p t -> p c t"),
            [("qD", s, p0)], [("qbuf", c) for c in range(4)], "qbuf")
        dma(cbbuf[:, :, :n], cbD[s][:, :, p0:p0 + n].rearrange("c p t -> p c t"),
            [("cbD", s, p0)], [("cbbuf", c) for c in range(4)], "cbbuf")
        ukeys = [("uD", s, t[1]) for t in tl if t[1] + t[2] >= p0 and t[1] <= p0 + n] + \
                [("uDpad", s, 0), ("uDpad", s, 1)]
        dma(ubuf[:, :, 0:n + 2], uD[s][:, :, p0:p0 + n + 2].rearrange("c p t -> p c t"),
            ukeys, [("ubuf", c) for c in range(4)], "ubuf")
        kchunks = [(0, NMETA, 0)] + [(NMETA + 128 * c, 128, 1 + c) for c in range(n_real[s] // 128)]
        ktile = {}
        for t in tl:
            for (k0, kn, ci) in kchunks:
                if t[1] <= k0 < t[1] + t[2]:
                    ktile[ci] = t[1]
        nu = len(kchunks)
        pending_post = [None]

        def post_a(po):
            fo = nxt("ft", NF)
            op("dve", "tensor_copy", [("ps", po)], [("ft", fo)], out=ft[fo][0:65, :n], in_=ps[po][0:65, :n])
            return fo

        def post_b(hd, fo):
            b = nxt("bt", NB)
            tt("dve", bt[b][0:65, :n], ft[fo][0:65, :n], ft[fo][0:65, :n], ALU.mult, [("ft", fo)], [("bt", b)])
            mm(ps[6][0:64, :n], cstb[0:65, CST_G65:CST_G65 + 64], bt[b][0:65, :n], True, True,
               [("bt", b), "cstb"], [("ps", 6)])
            fr = rstd_from_ps6(n, 64, False)
            stt("dve", yatt[:, hd, :n], ft[fo][0:64, :n], vcol("ga", l * 8 + hd, 64), ft[fr][0:64, :n],
                ALU.mult, ALU.mult, [("ft", fo), ("ft", fr), "vecs"], [("yatt", hd)])

        def conv_chunk(cc):
            f = nxt("ft", NF)
            cwb = (l * 4 + cc) * 3
            fk = ("ft", f)
            op("pool", "tensor_scalar", [("ubuf", cc), "vecs"], [fk], out=ft[f][:, :n], in0=ubuf[:, cc, 1:n + 1],
               scalar1=vcol("cw", cwb + 1), scalar2=vcol("cb", l * 4 + cc), op0=ALU.mult, op1=ALU.add)
            stt("dve", ft[f][:, :n], ubuf[:, cc, 0:n], vcol("cw", cwb), ft[f][:, :n], ALU.mult, ALU.add,
                [("ubuf", cc), "vecs", fk], [fk])
            stt("dve", ft[f][:, :n], ubuf[:, cc, 2:n + 2], vcol("cw", cwb + 2), ft[f][:, :n], ALU.mult, ALU.add,
                [("ubuf", cc), "vecs", fk], [fk])
            tt("pool", ft[f][:, :n], ft[f][:, :n], cbbuf[:, cc, :n], ALU.mult, [fk, ("cbbuf", cc)], [fk])
            fr = group_rstd(ft[f][:, :n], fk, n, 128, cstb[:, CST_BD64:CST_BD64 + 128], 128, True)
            stt("dve", yconv[:, cc, :n], ft[f][:, :n], vcol("gc", l * 4 + cc), ft[fr][:, :n], ALU.mult, ALU.mult,
                [fk, ("ft", fr), "vecs"], [("yconv", cc)])

        conv_done = [False] * 4
        for j in range(4):
            pis = [None] * nu

            def qk_exp(t):
                k0, kn, ci = kchunks[t]
                sidx = t % 2
                for g in range(2):
                    op("pe", "matmul", [("KT", s, ktile[ci]), ("qbuf", j)], [("ps", 2 * sidx + g)],
                       a=(psS[sidx][0:kn, g * TT:g * TT + n],), lhsT=KT[s][g * 64:(g + 1) * 64, k0:k0 + kn],
                       rhs=qbuf[g * 64:(g + 1) * 64, j, :n], start=True, stop=True, tile_position=(g * 64, 0))
                pi = nxt("pT", NP)
                pis[t] = pi
                src = psS[sidx][0:kn, :].rearrange("p (c t) -> p c t", c=2)[:, :, 0:n]
                actf(pT[pi][0:kn, :, :n], src, AF.Exp, [("ps", 2 * sidx), ("ps", 2 * sidx + 1), "negb"],
                     [("pT", pi)], bias=negb[0:kn, l:l + 1], scale=0.125)

            def pv(t):
                k0, kn, ci = kchunks[t]
                pi = pis[t]
                for g in range(2):
                    mm(ps[4 + g][0:65, :n], V[s][0:kn, ci, g * 65:(g + 1) * 65], pT[pi][0:kn, g, :n],
                       t == 0, t == nu - 1, [("V", s, ktile[ci]), ("Vones", s), ("pT", pi)], [("ps", 4 + g)])
                for _ in range(NDUMMY):
                    mm(ps[7][:, :n], cstb[:, CST_ONESD:CST_ONESD + 128], xn[:, 0, :n], True, True, ["cstb"], [])

            qk_exp(0)
            if nu > 1:
                qk_exp(1)
            for t in range(nu):
                if t + 2 < nu:
                    qk_exp(t + 2)
                pv(t)
                if t == 1 and pending_post[0] is not None:
                    pending_post[0]()
                    pending_post[0] = None
                if t == 3:
                    conv_chunk(j)
                    conv_done[j] = True
            if pending_post[0] is not None:
                pending_post[0]()
            if not conv_done[j]:
                conv_chunk(j)
                conv_done[j] = True
            fa = post_a(4); fb = post_a(5)
            pending_post[0] = (lambda j=j, fa=fa, fb=fb: (post_b(j, fa), post_b(4 + j, fb)))
        pending_post[0]()
        for oc in range(8):
            sl = wload("woa", l, oc, 1024, parts=64, col0=0)
            wload("woc", l, oc, 512, col0=1024, newslot=False)
            pq = oc % 2
            for cc in range(4):
                mm(ps[pq][:, :n], wslot[sl][:, 1024 + cc * 128:1024 + (cc + 1) * 128], yconv[:, cc, :n], cc == 0,
                   False, [("w", sl, 1024), ("yconv", cc)], [("ps", pq)])
            for ih, hd in enumerate((0, 4, 1, 5, 2, 6, 3, 7)):
                mm(ps[pq][:, :n], wslot[sl][0:64, hd * 128:(hd + 1) * 128], yatt[:, hd, :n], False, ih == 7,
                   [("w", sl, 0), ("yatt", hd)], [("ps", pq)])
            tt("dve", h[:, oc, :n], ps[pq][:, :n], h[:, oc, :n], ALU.add, [("ps", pq), ("h", oc)], [("h", oc)])
        ffn(n, l, 1)
        if not last:
            store_h(s, p0, n)
        elif r0 is not None:
            mean_sq_all(n)
            f = rstd_from_ps6(n, 128, True)
            for kc in range(KC):
                stt("dve", h[:, kc, :n], h[:, kc, :n], vcol("gf", kc), ft[f][:, :n], ALU.mult, ALU.mult,
                    [("h", kc), ("ft", f), "vecs"], [("h", kc)])
            dma(yT[s][:, :, r0:r0 + n].rearrange("c p t -> p c t"), h[:, :, :n], HK, [("yT", s, r0)], "h")

    for l in range(depth):
        for it, t in enumerate(all_tiles):
            nt = all_tiles[it + 1] if it + 1 < len(all_tiles) else None
            pass_a(l, t, nt, prenormed=(it > 0))
        for t in all_tiles:
            pass_b(l, t, l == depth - 1 and not debug)
    S.emit(final_sems=["h"])
    return nc


def host_prep(inp, depth):
    f32 = np.float32

    def blk(w, kc, m_idx=None):
        dpt, K, M = w.shape
        a = w.reshape(dpt, kc, 128, M // 128, 128)
        return np.ascontiguousarray(a.transpose(0, 3, 2, 1, 4)).reshape(dpt, M // 128, 128, kc * 128)

    out = {}
    for f, (g, u, d) in enumerate((("ffn1_w_gate", "ffn1_w_up", "ffn1_w_down"),
                                   ("ffn2_w_gate", "ffn2_w_up", "ffn2_w_down"))):
        gb = blk(np.asarray(inp[g], f32), 8); ub = blk(np.asarray(inp[u], f32), 8)
        out[f"gu{f}"] = np.ascontiguousarray(np.concatenate([gb, ub], axis=3))
        out[f"dn{f}"] = blk(np.asarray(inp[d], f32), 22)
    w_in = np.asarray(inp["w_in"], f32)
    cols = []
    for oc in range(4):
        cols += list(range(oc * 64, (oc + 1) * 64)) + list(range((4 + oc) * 64, (5 + oc) * 64))
    cols += list(range(512, 2304))
    out["win"] = blk(np.ascontiguousarray(w_in[:, :, cols]), 8)
    w_out = np.asarray(inp["w_out"], f32)
    wa = w_out[:, :512, :].reshape(depth, 8, 64, 8, 128)
    out["woa"] = np.ascontiguousarray(wa.transpose(0, 3, 2, 1, 4)).reshape(depth, 8, 64, 1024)
    out["woc"] = blk(np.ascontiguousarray(w_out[:, 512:, :]), 4)
    voff, NV = vec_layout(depth)
    vecs = np.zeros((128, NV), f32)

    def put(name, arr):
        vecs[:, voff[name]:voff[name] + arr.shape[1]] = arr

    def cm(v):
        dpt = v.shape[0]
        return np.ascontiguousarray(v.reshape(dpt, -1, 128).transpose(2, 0, 1)).reshape(128, -1)

    put("g1", cm(np.asarray(inp["ffn1_norm"], f32)))
    put("gm", cm(np.asarray(inp["mix_norm"], f32)))
    put("g2", cm(np.asarray(inp["ffn2_norm"], f32)))
    put("gf", cm(np.asarray(inp["final_norm"], f32)[None]))
    qn = np.asarray(inp["q_norm"], f32); kn = np.asarray(inp["k_norm"], f32)
    put("gq", np.concatenate([qn.T, qn.T], axis=0))
    put("gk", np.concatenate([kn.T, kn.T], axis=0))
    cw = np.asarray(inp["conv_w"], f32)
    put("cw", np.ascontiguousarray(cw.reshape(depth, 3, 4, 128).transpose(3, 0, 2, 1)).reshape(128, depth * 12))
    put("cb", cm(np.asarray(inp["conv_b"], f32)))
    ga = np.asarray(inp["attn_out_norm"], f32).reshape(depth, 8, 64)
    gav = np.zeros((128, depth * 8), f32)
    gav[:64] = ga.transpose(2, 0, 1).reshape(64, depth * 8)
    put("ga", gav)
    put("gc", cm(np.asarray(inp["conv_out_norm"], f32)))
    vecs[:, voff["eps"]] = EPS
    out["vecs"] = vecs
    out["gqk"] = np.ascontiguousarray(np.concatenate([qn, kn], axis=1).reshape(1, depth * 128))
    return out


def host_consts(nmax):
    f32 = np.float32
    cst = np.zeros((128, CST_N), f32)
    for i in range(64):
        cst[2 * i, CST_RT + 2 * i + 1] = 1.0
        cst[2 * i + 1, CST_RT + 2 * i] = -1.0
    cst[:, CST_ONESD:CST_ONESD + 128] = 1.0 / D
    cst[0:64, CST_BD64:CST_BD64 + 64] = 1.0 / 64
    cst[64:128, CST_BD64 + 64:CST_BD64 + 128] = 1.0 / 64
    cst[0:64, CST_G65:CST_G65 + 64] = 1.0 / 64
    cst[64, CST_G65:CST_G65 + 64] = EPS
    cst[:, CST_ONES:CST_ONES + 128] = 1.0
    rows = nmax // 64
    row = np.repeat(np.arange(rows, dtype=f32), 64)
    col = np.tile(np.arange(64, dtype=f32), rows)
    row = np.concatenate([np.zeros(NMETA, f32), row]); col = np.concatenate([np.zeros(NMETA, f32), col])
    freqs = (f32(10000.0) ** (-np.arange(16, dtype=f32) / f32(16))).astype(f32)
    ang = np.concatenate([row[:, None] * freqs, col[:, None] * freqs], axis=-1).astype(f32)
    cos = np.cos(ang).astype(f32); sin = np.sin(ang).astype(f32)
    pair = (np.arange(128) % 64) // 2
    cosT = np.ascontiguousarray(cos[:, pair].T); sinT = np.ascontiguousarray(sin[:, pair].T)
    return cst, cosT, sinT


_CACHE = {}


def run(xs_list, inp, depth):
    n_real = tuple(x.shape[0] for x in xs_list[0])
    key = (n_real, depth)
    if key not in _CACHE:
        _CACHE[key] = build(n_real, depth)
    nc = _CACHE[key]
    hp = host_prep(inp, depth)
    cst, cosT, sinT = host_consts(max(n_real))
    meta = np.asarray(inp["meta_tokens"], np.float32)
    metaT = np.ascontiguousarray(meta.T).reshape(KC, 128, NMETA)
    in_maps = []
    for xs in xs_list:
        m = dict(hp)
        m["cst"] = cst; m["cosT"] = cosT; m["sinT"] = sinT; m["metaT"] = metaT
        for s, x in enumerate(xs):
            m[f"xT{s}"] = np.ascontiguousarray(np.asarray(x, np.float32).T).reshape(KC, 128, x.shape[0])
        in_maps.append(m)
    res = run_bass_kernel_spmd(nc, in_maps, core_ids=list(range(len(xs_list))))
    outs = []
    for r in res.results:
        outs.append([np.ascontiguousarray(r[f"yT{s}"].reshape(D, -1).T) for s in range(len(n_real))])
    return outs


def kernel(x_prompt, x_sample, meta_tokens, **params):
    x_prompt = np.asarray(x_prompt, np.float32); x_sample = np.asarray(x_sample, np.float32)
    inp = dict(params); inp["meta_tokens"] = meta_tokens
    depth = np.asarray(params["w_in"]).shape[0]
    xs_list = [[x_sample[c], x_prompt[c % x_prompt.shape[0]]] for c in range(NCORES)]
    outs = run(xs_list, inp, depth)
    y_sample = np.stack([outs[c][0] for c in range(NCORES)], axis=0)
    y_prompt = np.stack([outs[c][1] for c in range(x_prompt.shape[0])], axis=0)
    return (y_prompt.astype(np.float32), y_sample.astype(np.float32))
```

```python
import numpy as np
import concourse.bass as bass
import concourse.mybir as mybir
from concourse.bass_utils import run_bass_kernel_spmd

F32 = mybir.dt.float32
BF16 = mybir.dt.bfloat16
AF = mybir.ActivationFunctionType
ALU = mybir.AluOpType

D = 1024
KC = 8
DFF = 2816
NJ = 22
NMETA = 16
TT = 512
EPS = 1e-6
NCORES = 8
SLOTW = 2816
NSLOT = 6
NDUMMY = 0
STQ = "sp"


class Op:
    __slots__ = ("eng", "fn", "deps", "flag", "count", "dsem", "dval", "idx", "pos")


class Sched:
    COMPUTE = ("pe", "act", "dve", "pool")
    SAME_ENG_WINDOW = 6

    def __init__(self, nc):
        self.nc = nc
        self.eobj = {"pe": nc.tensor, "act": nc.scalar, "dve": nc.vector, "pool": nc.gpsimd, "sp": nc.sync}
        self.esem = {e: nc.alloc_semaphore("S_" + e) for e in self.COMPUTE}
        self.streams = {e: [] for e in self.eobj}
        self.last_writer = {}
        self.readers = {}
        self.dsems = {}
        self.dvals = {}
        self.dlast = {}
        self.nops = 0

    def op(self, eng, meth, reads=(), writes=(), dsem=None, a=(), **kw):
        o = Op()
        o.eng = eng; o.fn = (meth, a, kw); o.flag = False; o.count = 0; o.dsem = None; o.dval = 0
        o.idx = self.nops; self.nops += 1
        o.pos = len(self.streams[eng])
        deps = {}
        lw = self.last_writer; rd = self.readers
        for k in reads:
            y = lw.get(k)
            if y is not None:
                deps[y.idx] = y
        for k in writes:
            y = lw.get(k)
            if y is not None:
                deps[y.idx] = y
            r = rd.get(k)
            if r:
                for y in r.values():
                    deps[y.idx] = y
        if dsem is not None:
            if dsem not in self.dsems:
                self.dsems[dsem] = self.nc.alloc_semaphore("D_" + str(len(self.dsems)))
                self.dvals[dsem] = 0
            prev = self.dlast.get(dsem)
            if prev is not None:
                deps[prev.idx] = prev
            self.dvals[dsem] += 16
            o.dsem = self.dsems[dsem]; o.dval = self.dvals[dsem]
            self.dlast[dsem] = o
        dl = []
        for y in deps.values():
            if y.dsem is not None:
                dl.append(y)
            elif y.eng == eng and o.dsem is None and (eng == "pe" or o.pos - y.pos > self.SAME_ENG_WINDOW):
                continue
            else:
                y.flag = True
                dl.append(y)
        o.deps = dl
        for k in reads:
            r = rd.get(k)
            if r is None:
                rd[k] = {eng: o}
            else:
                r[eng] = o
        for k in writes:
            lw[k] = o
            rd[k] = {}
        self.streams[eng].append(o)
        return o

    def emit(self, final_sems=()):
        nc = self.nc
        for e in self.COMPUTE:
            c = 0
            for o in self.streams[e]:
                if o.dsem is None and o.flag:
                    c += 1
                    o.count = c
        esem = self.esem
        with nc.Block() as block:
            def run(e, eng):
                waited = {}
                for o in self.streams[e]:
                    need = {}
                    for y in o.deps:
                        if y.dsem is not None:
                            s, v = y.dsem, y.dval
                        else:
                            s, v = esem[y.eng], y.count
                        if need.get(s, (None, 0))[1] < v:
                            need[s] = (s, v)
                    for s, v in need.values():
                        if waited.get(s, 0) < v:
                            eng.wait_ge(s, v)
                            waited[s] = v
                    meth, a, kw = o.fn
                    ins = getattr(eng, meth)(*a, **kw)
                    if o.dsem is not None:
                        ins.then_inc(o.dsem, 16)
                    elif o.flag:
                        ins.then_inc(esem[e], 1)
                if e == "sp":
                    for k in final_sems:
                        eng.wait_ge(self.dsems[k], self.dvals[k])

            @block.sync
            def _(eng):
                run("sp", eng)

            @block.tensor
            def _(eng):
                run("pe", eng)

            @block.scalar
            def _(eng):
                run("act", eng)

            @block.vector
            def _(eng):
                run("dve", eng)

            @block.gpsimd
            def _(eng):
                run("pool", eng)


def vec_layout(depth):
    off = {}
    c = 0
    for name, w in (("g1", depth * 8), ("gm", depth * 8), ("g2", depth * 8), ("gf", 8),
                    ("gq", depth), ("gk", depth), ("cw", depth * 12), ("cb", depth * 4),
                    ("ga", depth * 8), ("gc", depth * 4), ("eps", 1)):
        off[name] = c
        c += w
    return off, c


CST_RT, CST_ONESD, CST_BD64, CST_G65, CST_ONES, CST_N = 0, 128, 256, 384, 448, 576


def build(n_real=(8192, 4096), depth=4, debug=False):
    nseq = len(n_real)
    L = [n + NMETA for n in n_real]
    nmax = max(n_real)
    voff, NV = vec_layout(depth)
    nc = bass.Bass("TRN2", target_bir_lowering=False)

    def din(name, shape, dt=F32):
        return nc.dram_tensor(name, list(shape), dt, kind="ExternalInput").ap()

    def dscr(name, shape, dt):
        return nc.dram_tensor(name, list(shape), dt, kind="Internal").ap()

    xT = [din(f"xT{s}", [KC, 128, n_real[s]]) for s in range(nseq)]
    metaT = din("metaT", [KC, 128, NMETA])
    wf = {"gu0": din("gu0", [depth, NJ, 128, 2048]), "gu1": din("gu1", [depth, NJ, 128, 2048]),
          "dn0": din("dn0", [depth, 8, 128, SLOTW]), "dn1": din("dn1", [depth, 8, 128, SLOTW]),
          "win": din("win", [depth, 18, 128, 1024]),
          "woa": din("woa", [depth, 8, 64, 1024]), "woc": din("woc", [depth, 8, 128, 512])}
    wb = {k: dscr(k + "_b", v.shape, BF16) for k, v in wf.items()}
    vecs_d = din("vecs", [128, NV])
    gqk_d = din("gqk", [1, depth * 128])
    cst_d = din("cst", [128, CST_N])
    cos_d = din("cosT", [128, NMETA + nmax])
    sin_d = din("sinT", [128, NMETA + nmax])
    yT = [nc.dram_tensor(f"yT{s}", [KC, 128, n_real[s]], F32, kind="ExternalOutput").ap() for s in range(nseq)]
    if debug:
        hD = [nc.dram_tensor(f"hD{s}", [KC, 128, L[s]], F32, kind="ExternalOutput").ap() for s in range(nseq)]
    else:
        hD = [dscr(f"hD{s}", [KC, 128, L[s]], F32) for s in range(nseq)]
    qD = [dscr(f"qD{s}", [4, 128, L[s]], BF16) for s in range(nseq)]
    cbD = [dscr(f"cbD{s}", [4, 128, L[s]], BF16) for s in range(nseq)]
    uD = [dscr(f"uD{s}", [4, 128, L[s] + 2], BF16) for s in range(nseq)]

    sb = nc.alloc_sbuf_tensor
    NCH = [1 + n // 128 for n in n_real]
    KT = [sb(f"KT{s}", [128, L[s]], BF16) for s in range(nseq)]
    V = [sb(f"V{s}", [128, NCH[s], 130], BF16) for s in range(nseq)]
    h = sb("h", [128, KC, TT], F32)
    xn = sb("xn", [128, KC, TT], BF16)
    xnb = sb("xnb", [128, KC, TT], BF16)
    act = sb("act", [128, NJ, TT], BF16)
    wslot = [sb(f"w{i}", [128, SLOTW], BF16) for i in range(NSLOT)]
    qbuf = sb("qbuf", [128, 4, TT], BF16)
    cbbuf = sb("cbbuf", [128, 4, TT], BF16)
    ubuf = sb("ubuf", [128, 4, TT + 2], BF16)
    cs = sb("cs", [128, 2, TT], F32)
    NF = 8
    ft = [sb(f"ft{i}", [128, TT], F32) for i in range(NF)]
    NB = 6
    bt = [sb(f"bt{i}", [128, TT], BF16) for i in range(NB)]
    NP = 4
    pT = [sb(f"pT{i}", [128, 2, TT], BF16) for i in range(NP)]
    yatt = sb("yatt", [64, 8, TT], BF16)
    yconv = sb("yconv", [128, 4, TT], BF16)
    vecs = sb("vecs_sb", [128, NV], F32)
    cstf = sb("cstf", [128, CST_N], F32)
    cstb = sb("cstb", [128, CST_N], BF16)
    gqk = sb("gqk_sb", [1, depth * 128], F32)
    smax = sb("smax", [1, 4], F32)
    gneg = sb("gneg_sb", [1, depth * 128], F32)
    negb = sb("negb", [128, depth], F32)
    zero = sb("zero", [128, 8], BF16)
    psS = [nc.alloc_psum_tensor(f"psS{i}", [128, 2 * TT], F32) for i in range(2)]
    ps = [psS[i // 2][:, (i % 2) * TT:(i % 2 + 1) * TT] for i in range(4)] + \
         [nc.alloc_psum_tensor(f"ps{i}", [128, TT], F32)[:, :] for i in range(4, 8)]

    S = Sched(nc)
    op = S.op
    AX = mybir.AxisListType.X

    def mm(out, lhsT, rhs, start, stop, reads, writes):
        op("pe", "matmul", reads, writes, a=(out,), lhsT=lhsT, rhs=rhs, start=start, stop=stop)

    def actf(out, in_, func, reads, writes, **kw):
        op("act", "activation", reads, writes, out=out, in_=in_, func=func, **kw)

    def tt(eng, out, in0, in1, alu, reads, writes):
        op(eng, "tensor_tensor", reads, writes, out=out, in0=in0, in1=in1, op=alu)

    def stt(eng, out, in0, scalar, in1, op0, op1, reads, writes):
        op(eng, "scalar_tensor_tensor", reads, writes, out=out, in0=in0, scalar=scalar, in1=in1, op0=op0, op1=op1)

    def recip(ap, key):
        op("dve", "reciprocal", [key], [key], out=ap, in_=ap)

    def dma(out, in_, reads, writes, dsem, eng="sp", **kw):
        op(eng, "dma_start", reads, writes, dsem=dsem, out=out, in_=in_, **kw)

    def vcol(name, c, parts=128):
        return vecs[0:parts, voff[name] + c: voff[name] + c + 1]

    HK = [("h", kc) for kc in range(KC)]

    dma(vecs[:], vecs_d, [], ["vecs"], "c0")
    dma(cstf[:], cst_d, [], ["cstf"], "c1")
    dma(gqk[:], gqk_d, [], ["gqk"], "c2")
    op("dve", "tensor_copy", ["cstf"], ["cstb"], out=cstb[:], in_=cstf[:])
    op("dve", "memset", [], ["zero"], a=(zero[:], 0.0))
    for s in range(nseq):
        op("dve", "memset", [], [("Vones", s)], a=(V[s][:, :, 64:65], 1.0))
        op("dve", "memset", [], [("Vones", s)], a=(V[s][:, :, 129:130], 1.0))
        for side in (0, 1):
            col = 0 if side == 0 else L[s] + 1
            dma(uD[s][:, :, col:col + 1].rearrange("c p o -> p c o"),
                zero[:, 0:4].rearrange("p (c o) -> p c o", o=1),
                ["zero"], [("uDpad", s, side)], ("pad", s, side), allow_slow_non_contiguous=True)
    for l in range(depth):
        gl = gqk[:, l * 128:(l + 1) * 128]
        gn = gneg[:, l * 128:(l + 1) * 128]
        op("dve", "tensor_scalar", ["gqk"], ["gneg"], out=gn, in0=gl, scalar1=-1.0, scalar2=None, op0=ALU.mult)
        tt("dve", gl, gl, gn, ALU.max, ["gqk", "gneg"], ["gqk"])
        op("dve", "reduce_max", ["gqk"], ["smax"], out=smax[:, 0:1], in_=gqk[:, l * 128:l * 128 + 64], axis=AX)
        op("dve", "reduce_max", ["gqk"], ["smax"], out=smax[:, 1:2], in_=gqk[:, l * 128 + 64:l * 128 + 128], axis=AX)
        tt("dve", smax[:, 2:3], smax[:, 0:1], smax[:, 1:2], ALU.mult, ["smax"], ["smax"])
        op("dve", "tensor_scalar", ["smax"], ["smax"], out=smax[:, 3:4], in0=smax[:, 2:3], scalar1=-8.0,
           scalar2=None, op0=ALU.mult)
        mm(ps[7][:, 0:1], cstf[0:1, CST_ONES:CST_ONES + 128], smax[0:1, 3:4], True, True,
           ["smax", "cstf"], [("ps", 7)])
        actf(negb[:, l:l + 1], ps[7][:, 0:1], AF.Copy, [("ps", 7)], ["negb"])

    ncast = [0]
    PER = {"gu0": 2, "gu1": 2, "dn0": 2, "dn1": 2, "win": 6, "woa": 8, "woc": 8}

    def cast(name, l):
        src = wf[name][l]; dst = wb[name][l]
        nb = src.shape[0]
        per = PER[name]
        for b0 in range(0, nb, per):
            b1 = min(nb, b0 + per)
            k = ncast[0] % (3 if (l == 0 and name in ("gu0", "dn0", "win")) else 8); ncast[0] += 1
            dma(dst[b0:b1].rearrange("b p x -> (b p) x"), src[b0:b1].rearrange("b p x -> (b p) x"),
                [], [("wb", name, l, b) for b in range(b0, b1)], ("cast", k), eng="pool")

    for l in range(depth):
        for name in ("gu0", "dn0", "win", "woa", "woc", "gu1", "dn1"):
            cast(name, l)

    ring = [0]

    def wload(name, l, b, width, parts=128, col0=0, newslot=True):
        if newslot:
            ring[0] += 1
        s = ring[0] % NSLOT
        wkeys = [("w", s, col0)] + ([("w", s, 1024)] if (col0 == 0 and width > 1024) else [])
        dma(wslot[s][0:parts, col0:col0 + width], wb[name][l, b], [("wb", name, l, b)], wkeys,
            ("w", s, col0))
        return s

    rr = {"ft": 0, "bt": 0, "pT": 0}

    def nxt(kind, n):
        rr[kind] = (rr[kind] + 1) % n
        return rr[kind]

    def mean_sq_all(n):
        for kc in range(KC):
            b = nxt("bt", NB)
            actf(bt[b][:, :n], h[:, kc, :n], AF.Square, [("h", kc)], [("bt", b)])
            mm(ps[6][:, :n], cstb[:, CST_ONESD:CST_ONESD + 128], bt[b][:, :n], kc == 0, kc == KC - 1,
               [("bt", b), "cstb"], [("ps", 6)])

    def rstd_from_ps6(n, parts, with_eps):
        f = nxt("ft", NF)
        if with_eps:
            actf(ft[f][0:parts, :n], ps[6][0:parts, :n], AF.Ln, [("ps", 6), "vecs"], [("ft", f)],
                 bias=vcol("eps", 0, parts))
        else:
            actf(ft[f][0:parts, :n], ps[6][0:parts, :n], AF.Ln, [("ps", 6)], [("ft", f)])
        actf(ft[f][0:parts, :n], ft[f][0:parts, :n], AF.Exp, [("ft", f)], [("ft", f)], scale=-0.5)
        return f

    def rms_to_xn(n, gname, gbase, dst=None, dkey="xn"):
        dst = xn if dst is None else dst
        mean_sq_all(n)
        f = rstd_from_ps6(n, 128, True)
        for kc in range(KC):
            stt("dve", dst[:, kc, :n], h[:, kc, :n], vcol(gname, gbase + kc), ft[f][:, :n], ALU.mult, ALU.mult,
                [("h", kc), ("ft", f), "vecs"], [(dkey, kc)])

    def ffn(n, l, which, prenormed=False, mid_hook=None, early_hook=None):
        if not prenormed:
            rms_to_xn(n, "g1" if which == 0 else "g2", l * 8)
        gu = "gu%d" % which; dn = "dn%d" % which
        for j in range(NJ):
            s = wload(gu, l, j, 2048)
            pg = j % 2; pu = 2 + j % 2
            for kc in range(KC):
                mm(ps[pg][:, :n], wslot[s][:, kc * 128:(kc + 1) * 128], xn[:, kc, :n], kc == 0, kc == KC - 1,
                   [("w", s, 0), ("xn", kc)], [("ps", pg)])
            for kc in range(KC):
                mm(ps[pu][:, :n], wslot[s][:, 1024 + kc * 128:1024 + (kc + 1) * 128], xn[:, kc, :n], kc == 0,
                   kc == KC - 1, [("w", s, 0), ("w", s, 1024), ("xn", kc)], [("ps", pu)])
            f = nxt("ft", NF)
            actf(ft[f][:, :n], ps[pg][:, :n], AF.Silu, [("ps", pg)], [("ft", f)])
            tt("dve", act[:, j, :n], ft[f][:, :n], ps[pu][:, :n], ALU.mult, [("ft", f), ("ps", pu)], [("act", j)])
            if j == 5 and early_hook is not None:
                early_hook()
        if mid_hook is not None:
            mid_hook()
        for i in range(KC):
            s = wload(dn, l, i, SLOTW)
            po = 4 + i % 2
            for j in range(NJ):
                mm(ps[po][:, :n], wslot[s][:, j * 128:(j + 1) * 128], act[:, j, :n], j == 0, j == NJ - 1,
                   [("w", s, 0), ("w", s, 1024), ("act", j)], [("ps", po)])
            stt("dve", h[:, i, :n], ps[po][:, :n], 0.5, h[:, i, :n], ALU.mult, ALU.add,
                [("ps", po), ("h", i)], [("h", i)])

    def group_rstd(src_ap, src_key, n, parts, lhsT_ap, out_parts, with_eps):
        b = nxt("bt", NB)
        actf(bt[b][0:parts, :n], src_ap, AF.Square, [src_key], [("bt", b)])
        mm(ps[6][0:out_parts, :n], lhsT_ap, bt[b][0:parts, :n], True, True, [("bt", b), "cstb"], [("ps", 6)])
        return rstd_from_ps6(n, out_parts, with_eps)

    def qk_epilogue(pq, n, l, is_q, dst_ap, dst_key):
        st = {}

        def stage1():
            f = group_rstd(ps[pq][:, :n], ("ps", pq), n, 128, cstb[:, CST_BD64:CST_BD64 + 128], 128, True)
            g = nxt("ft", NF)
            stt("dve", ft[g][:, :n], ps[pq][:, :n], vcol("gq" if is_q else "gk", l), ft[f][:, :n], ALU.mult,
                ALU.mult, [("ps", pq), ("ft", f), "vecs"], [("ft", g)])
            st["f"] = f; st["g"] = g

        def stage2():
            f = st["f"]; g = st["g"]
            mm(ps[7][:, :n], cstf[:, CST_RT:CST_RT + 128], ft[g][:, :n], True, True, [("ft", g), "cstf"], [("ps", 7)])
            tt("dve", ft[f][:, :n], ps[7][:, :n], cs[:, 1, :n], ALU.mult, [("ps", 7), "cs1"], [("ft", f)])
            tt("pool", ft[g][:, :n], ft[g][:, :n], cs[:, 0, :n], ALU.mult, [("ft", g), "cs0"], [("ft", g)])
            tt("dve", dst_ap, ft[f][:, :n], ft[g][:, :n], ALU.add, [("ft", f), ("ft", g)], [dst_key])

        return [stage1, stage2]

    def tiles_of(s):
        t = [(s, 0, NMETA, None)]
        for r0 in range(0, n_real[s], TT):
            t.append((s, NMETA + r0, min(TT, n_real[s] - r0), r0))
        return t

    all_tiles = []
    for s in range(nseq):
        all_tiles += tiles_of(s)

    def load_h(s, p0, n, r0, l):
        if l == 0:
            src = metaT if r0 is None else xT[s][:, :, r0:r0 + n]
            dma(h[:, :, :n], src.rearrange("c p t -> p c t"), [], HK, "h")
        else:
            dma(h[:, :, :n], hD[s][:, :, p0:p0 + n].rearrange("c p t -> p c t"), [("hD", s, p0)], HK, "h")

    def store_h(s, p0, n):
        dma(hD[s][:, :, p0:p0 + n].rearrange("c p t -> p c t"), h[:, :, :n], HK, [("hD", s, p0)], "h", eng=STQ)

    WIN_ORDER = [0, 1, 2, 3, 4, 5, 6, 7, 8, 9, 10, 14, 11, 15, 12, 16, 13, 17]

    def pass_a(l, tile, nxt_tile=None, prenormed=False, prev_stores=None):
        s, p0, n, r0 = tile
        if not prenormed:
            load_h(s, p0, n, r0, l)
        dma(cs[:, 0, :n], cos_d[:, p0:p0 + n], [], ["cs0"], "cs0")
        dma(cs[:, 1, :n], sin_d[:, p0:p0 + n], [], ["cs1"], "cs1")
        ffn(n, l, 0, prenormed, early_hook=prev_stores)
        store_h(s, p0, n)
        rms_to_xn(n, "gm", l * 8, dst=xnb, dkey="xnb")
        pending_cc = {}
        sl = None
        queue = []
        for idx, oc in enumerate(WIN_ORDER):
            half = idx % 2
            if half == 0:
                sl = wload("win", l, oc, 1024, col0=0)
                wload("win", l, WIN_ORDER[idx + 1], 1024, col0=1024, newslot=False)
            c0 = half * 1024
            wk = ("w", sl, c0)
            if oc == 5:
                for item in list(queue):
                    item.pop(0)()
                    if not item:
                        queue.remove(item)
                nblk = (n + 127) // 128
                for b in range(nblk):
                    tn = min(128, n - b * 128)
                    for kc in range(KC):
                        mm(ps[7][0:tn, b * 128:(b + 1) * 128], xnb[:, kc, b * 128:b * 128 + tn],
                           wslot[sl][:, c0 + kc * 128:c0 + (kc + 1) * 128], kc == 0, kc == KC - 1,
                           [wk, ("xnb", kc)], [("ps", 7)])
                ch0 = 0 if r0 is None else 1 + r0 // 128
                tn = min(128, n)
                for g in range(2):
                    src = ps[7][0:tn, 0:nblk * 128].rearrange("p (b c) -> p b c", c=128)[:, :, g * 64:(g + 1) * 64]
                    actf(V[s][0:tn, ch0:ch0 + nblk, g * 65:g * 65 + 64], src, AF.Copy, [("ps", 7)], [("V", s, p0)])
                continue
            if idx == 10 and nxt_tile is not None:
                s2, p02, n2, r02 = nxt_tile
                load_h(s2, p02, n2, r02, l)
                rms_to_xn(n2, "g1", l * 8)
            pq = idx % 4
            for kc in range(KC):
                mm(ps[pq][:, :n], wslot[sl][:, c0 + kc * 128:c0 + (kc + 1) * 128], xnb[:, kc, :n], kc == 0,
                   kc == KC - 1, [wk, ("xnb", kc)], [("ps", pq)])
            for item in list(queue):
                item.pop(0)()
                if not item:
                    queue.remove(item)
            if oc < 4:
                queue.append(qk_epilogue(pq, n, l, True, qbuf[:, oc, :n], ("qbuf", oc)))
            elif oc == 4:
                queue.append(qk_epilogue(pq, n, l, False, KT[s][:, p0:p0 + n], ("KT", s, p0)))
            elif oc < 10:
                cc = oc - 6
                actf(cbbuf[:, cc, :n], ps[pq][:, :n], AF.Copy, [("ps", pq)], [("cbbuf", cc)])
            elif oc < 14:
                f = nxt("ft", NF)
                pending_cc[oc - 10] = f
                actf(ft[f][:, :n], ps[pq][:, :n], AF.Copy, [("ps", pq)], [("ft", f)])
            else:
                cc = oc - 14
                f = pending_cc[cc]
                tt("dve", ubuf[:, cc, 1:n + 1], ft[f][:, :n], ps[pq][:, :n], ALU.mult,
                   [("ft", f), ("ps", pq)], [("ubuf", cc)])
        for item in queue:
            for fn in item:
                fn()
        def stores():
            dma(qD[s][:, :, p0:p0 + n].rearrange("c p t -> p c t"), qbuf[:, :, :n],
                [("qbuf", c) for c in range(4)], [("qD", s, p0)], "qbuf", eng=STQ)
            dma(cbD[s][:, :, p0:p0 + n].rearrange("c p t -> p c t"), cbbuf[:, :, :n],
                [("cbbuf", c) for c in range(4)], [("cbD", s, p0)], "cbbuf", eng=STQ)
            dma(uD[s][:, :, 1 + p0:1 + p0 + n].rearrange("c p t -> p c t"), ubuf[:, :, 1:n + 1],
                [("ubuf", c) for c in range(4)], [("uD", s, p0)], "ubuf", eng=STQ)
        return stores

    def load_qcu(tile):
        s, p0, n, r0 = tile
        tl = tiles_of(s)
        dma(qbuf[:, :, :n], qD[s][:, :, p0:p0 + n].rearrange("c p t -> p c t"),
            [("qD", s, p0)], [("qbuf", c) for c in range(4)], "qbuf")
        dma(cbbuf[:, :, :n], cbD[s][:, :, p0:p0 + n].rearrange("c p t -> p c t"),
            [("cbD", s, p0)], [("cbbuf", c) for c in range(4)], "cbbuf")
        ukeys = [("uD", s, t[1]) for t in tl if t[1] + t[2] >= p0 and t[1] <= p0 + n] + \
                [("uDpad", s, 0), ("uDpad", s, 1)]
        dma(ubuf[:, :, 0:n + 2], uD[s][:, :, p0:p0 + n + 2].rearrange("c p t -> p c t"),
            ukeys, [("ubuf", c) for c in range(4)], "ubuf")

    def pass_b(l, tile, last, nxt_tile=None, preloaded=False):
        s, p0, n, r0 = tile
        tl = tiles_of(s)
        if not preloaded:
            load_qcu(tile)
        kchunks = [(0, NMETA, 0)] + [(NMETA + 128 * c, 128, 1 + c) for c in range(n_real[s] // 128)]
        ktile = {}
        for t in tl:
            for (k0, kn, ci) in kchunks:
                if t[1] <= k0 < t[1] + t[2]:
                    ktile[ci] = t[1]
        nu = len(kchunks)
        pending_post = [None]

        def post_a(po):
            fo = nxt("ft", NF)
            op("dve", "tensor_copy", [("ps", po)], [("ft", fo)], out=ft[fo][0:65, :n], in_=ps[po][0:65, :n])
            return fo

        def post_sq(fo):
            b = nxt("bt", NB)
            tt("dve", bt[b][0:65, :n], ft[fo][0:65, :n], ft[fo][0:65, :n], ALU.mult, [("ft", fo)], [("bt", b)])
            return b

        def post_b(hd, fo, b):
            mm(ps[6][0:64, :n], cstb[0:65, CST_G65:CST_G65 + 64], bt[b][0:65, :n], True, True,
               [("bt", b), "cstb"], [("ps", 6)])
            fr = rstd_from_ps6(n, 64, False)
            stt("dve", yatt[:, hd, :n], ft[fo][0:64, :n], vcol("ga", l * 8 + hd, 64), ft[fr][0:64, :n],
                ALU.mult, ALU.mult, [("ft", fo), ("ft", fr), "vecs"], [("yatt", hd)])

        def conv_chunk(cc):
            f = nxt("ft", NF)
            cwb = (l * 4 + cc) * 3
            fk = ("ft", f)
            op("pool", "tensor_scalar", [("ubuf", cc), "vecs"], [fk], out=ft[f][:, :n], in0=ubuf[:, cc, 1:n + 1],
               scalar1=vcol("cw", cwb + 1), scalar2=vcol("cb", l * 4 + cc), op0=ALU.mult, op1=ALU.add)
            stt("dve", ft[f][:, :n], ubuf[:, cc, 0:n], vcol("cw", cwb), ft[f][:, :n], ALU.mult, ALU.add,
                [("ubuf", cc), "vecs", fk], [fk])
            stt("dve", ft[f][:, :n], ubuf[:, cc, 2:n + 2], vcol("cw", cwb + 2), ft[f][:, :n], ALU.mult, ALU.add,
                [("ubuf", cc), "vecs", fk], [fk])
            tt("pool", ft[f][:, :n], ft[f][:, :n], cbbuf[:, cc, :n], ALU.mult, [fk, ("cbbuf", cc)], [fk])
            fr = group_rstd(ft[f][:, :n], fk, n, 128, cstb[:, CST_BD64:CST_BD64 + 128], 128, True)
            stt("dve", yconv[:, cc, :n], ft[f][:, :n], vcol("gc", l * 4 + cc), ft[fr][:, :n], ALU.mult, ALU.mult,
                [fk, ("ft", fr), "vecs"], [("yconv", cc)])

        conv_done = [False] * 4
        for j in range(4):
            pis = [None] * nu

            def qk_exp(t):
                k0, kn, ci = kchunks[t]
                sidx = t % 2
                for g in range(2):
                    op("pe", "matmul", [("KT", s, ktile[ci]), ("qbuf", j)], [("ps", 2 * sidx + g)],
                       a=(psS[sidx][0:kn, g * TT:g * TT + n],), lhsT=KT[s][g * 64:(g + 1) * 64, k0:k0 + kn],
                       rhs=qbuf[g * 64:(g + 1) * 64, j, :n], start=True, stop=True, tile_position=(g * 64, 0))
                pi = nxt("pT", NP)
                pis[t] = pi
                src = psS[sidx][0:kn, :].rearrange("p (c t) -> p c t", c=2)[:, :, 0:n]
                actf(pT[pi][0:kn, :, :n], src, AF.Exp, [("ps", 2 * sidx), ("ps", 2 * sidx + 1), "negb"],
                     [("pT", pi)], bias=negb[0:kn, l:l + 1], scale=0.125)

            def pv(t):
                k0, kn, ci = kchunks[t]
                pi = pis[t]
                for g in range(2):
                    mm(ps[4 + g][0:65, :n], V[s][0:kn, ci, g * 65:(g + 1) * 65], pT[pi][0:kn, g, :n],
                       t == 0, t == nu - 1, [("V", s, ktile[ci]), ("Vones", s), ("pT", pi)], [("ps", 4 + g)])
                for _ in range(NDUMMY):
                    mm(ps[7][:, :n], cstb[:, CST_ONESD:CST_ONESD + 128], xn[:, 0, :n], True, True, ["cstb"], [])

            qk_exp(0)
            if nu > 1:
                qk_exp(1)
            for t in range(nu):
                if t + 2 < nu:
                    qk_exp(t + 2)
                pv(t)
                if t == min(8, nu - 1) and pending_post[0] is not None:
                    pending_post[0]()
                    pending_post[0] = None
                if t == 3:
                    conv_chunk(j)
                    conv_done[j] = True
            if pending_post[0] is not None:
                pending_post[0]()
            if not conv_done[j]:
                conv_chunk(j)
                conv_done[j] = True
            fa = post_a(4); fb = post_a(5)
            ba = post_sq(fa); bb = post_sq(fb)
            pending_post[0] = (lambda j=j, fa=fa, fb=fb, ba=ba, bb=bb: (post_b(j, fa, ba), post_b(4 + j, fb, bb)))
        pending_post[0]()
        load_h(s, p0, n, r0, 1)
        for oc in range(8):
            sl = wload("woa", l, oc, 1024, parts=64, col0=0)
            wload("woc", l, oc, 512, col0=1024, newslot=False)
            pq = oc % 2
            for cc in range(4):
                mm(ps[pq][:, :n], wslot[sl][:, 1024 + cc * 128:1024 + (cc + 1) * 128], yconv[:, cc, :n], cc == 0,
                   False, [("w", sl, 1024), ("yconv", cc)], [("ps", pq)])
            for ih, hd in enumerate((0, 4, 1, 5, 2, 6, 3, 7)):
                mm(ps[pq][:, :n], wslot[sl][0:64, hd * 128:(hd + 1) * 128], yatt[:, hd, :n], False, ih == 7,
                   [("w", sl, 0), ("yatt", hd)], [("ps", pq)])
            tt("dve", h[:, oc, :n], ps[pq][:, :n], h[:, oc, :n], ALU.add, [("ps", pq), ("h", oc)], [("h", oc)])
        ffn(n, l, 1, mid_hook=(None if nxt_tile is None else (lambda: load_qcu(nxt_tile))))
        if not last:
            store_h(s, p0, n)
        elif r0 is not None:
            mean_sq_all(n)
            f = rstd_from_ps6(n, 128, True)
            for kc in range(KC):
                stt("dve", h[:, kc, :n], h[:, kc, :n], vcol("gf", kc), ft[f][:, :n], ALU.mult, ALU.mult,
                    [("h", kc), ("ft", f), "vecs"], [("h", kc)])
            dma(yT[s][:, :, r0:r0 + n].rearrange("c p t -> p c t"), h[:, :, :n], HK, [("yT", s, r0)], "h", eng=STQ)

    for l in range(depth):
        pst = None
        for it, t in enumerate(all_tiles):
            nt = all_tiles[it + 1] if it + 1 < len(all_tiles) else None
            pst = pass_a(l, t, nt, prenormed=(it > 0), prev_stores=pst)
        pst()
        for it, t in enumerate(all_tiles):
            nt = all_tiles[it + 1] if it + 1 < len(all_tiles) else None
            pass_b(l, t, l == depth - 1 and not debug, nt, preloaded=(it > 0))
    S.emit(final_sems=["h"])
    return nc


def host_prep(inp, depth):
    f32 = np.float32

    def blk(w, kc, m_idx=None):
        dpt, K, M = w.shape
        a = w.reshape(dpt, kc, 128, M // 128, 128)
        return np.ascontiguousarray(a.transpose(0, 3, 2, 1, 4)).reshape(dpt, M // 128, 128, kc * 128)

    out = {}
    for f, (g, u, d) in enumerate((("ffn1_w_gate", "ffn1_w_up", "ffn1_w_down"),
                                   ("ffn2_w_gate", "ffn2_w_up", "ffn2_w_down"))):
        gb = blk(np.asarray(inp[g], f32), 8); ub = blk(np.asarray(inp[u], f32), 8)
        out[f"gu{f}"] = np.ascontiguousarray(np.concatenate([gb, ub], axis=3))
        out[f"dn{f}"] = blk(np.asarray(inp[d], f32), 22)
    w_in = np.asarray(inp["w_in"], f32)
    cols = []
    for oc in range(4):
        cols += list(range(oc * 64, (oc + 1) * 64)) + list(range((4 + oc) * 64, (5 + oc) * 64))
    cols += list(range(512, 2304))
    out["win"] = blk(np.ascontiguousarray(w_in[:, :, cols]), 8)
    w_out = np.asarray(inp["w_out"], f32)
    wa = w_out[:, :512, :].reshape(depth, 8, 64, 8, 128)
    out["woa"] = np.ascontiguousarray(wa.transpose(0, 3, 2, 1, 4)).reshape(depth, 8, 64, 1024)
    out["woc"] = blk(np.ascontiguousarray(w_out[:, 512:, :]), 4)
    voff, NV = vec_layout(depth)
    vecs = np.zeros((128, NV), f32)

    def put(name, arr):
        vecs[:, voff[name]:voff[name] + arr.shape[1]] = arr

    def cm(v):
        dpt = v.shape[0]
        return np.ascontiguousarray(v.reshape(dpt, -1, 128).transpose(2, 0, 1)).reshape(128, -1)

    put("g1", cm(np.asarray(inp["ffn1_norm"], f32)))
    put("gm", cm(np.asarray(inp["mix_norm"], f32)))
    put("g2", cm(np.asarray(inp["ffn2_norm"], f32)))
    put("gf", cm(np.asarray(inp["final_norm"], f32)[None]))
    qn = np.asarray(inp["q_norm"], f32); kn = np.asarray(inp["k_norm"], f32)
    put("gq", np.concatenate([qn.T, qn.T], axis=0))
    put("gk", np.concatenate([kn.T, kn.T], axis=0))
    cw = np.asarray(inp["conv_w"], f32)
    put("cw", np.ascontiguousarray(cw.reshape(depth, 3, 4, 128).transpose(3, 0, 2, 1)).reshape(128, depth * 12))
    put("cb", cm(np.asarray(inp["conv_b"], f32)))
    ga = np.asarray(inp["attn_out_norm"], f32).reshape(depth, 8, 64)
    gav = np.zeros((128, depth * 8), f32)
    gav[:64] = ga.transpose(2, 0, 1).reshape(64, depth * 8)
    put("ga", gav)
    put("gc", cm(np.asarray(inp["conv_out_norm"], f32)))
    vecs[:, voff["eps"]] = EPS
    out["vecs"] = vecs
    out["gqk"] = np.ascontiguousarray(np.concatenate([qn, kn], axis=1).reshape(1, depth * 128))
    return out


def host_consts(nmax):
    f32 = np.float32
    cst = np.zeros((128, CST_N), f32)
    for i in range(64):
        cst[2 * i, CST_RT + 2 * i + 1] = 1.0
        cst[2 * i + 1, CST_RT + 2 * i] = -1.0
    cst[:, CST_ONESD:CST_ONESD + 128] = 1.0 / D
    cst[0:64, CST_BD64:CST_BD64 + 64] = 1.0 / 64
    cst[64:128, CST_BD64 + 64:CST_BD64 + 128] = 1.0 / 64
    cst[0:64, CST_G65:CST_G65 + 64] = 1.0 / 64
    cst[64, CST_G65:CST_G65 + 64] = EPS
    cst[:, CST_ONES:CST_ONES + 128] = 1.0
    rows = nmax // 64
    row = np.repeat(np.arange(rows, dtype=f32), 64)
    col = np.tile(np.arange(64, dtype=f32), rows)
    row = np.concatenate([np.zeros(NMETA, f32), row]); col = np.concatenate([np.zeros(NMETA, f32), col])
    freqs = (f32(10000.0) ** (-np.arange(16, dtype=f32) / f32(16))).astype(f32)
    ang = np.concatenate([row[:, None] * freqs, col[:, None] * freqs], axis=-1).astype(f32)
    cos = np.cos(ang).astype(f32); sin = np.sin(ang).astype(f32)
    pair = (np.arange(128) % 64) // 2
    cosT = np.ascontiguousarray(cos[:, pair].T); sinT = np.ascontiguousarray(sin[:, pair].T)
    return cst, cosT, sinT


_CACHE = {}


def run(xs_list, inp, depth):
    n_real = tuple(x.shape[0] for x in xs_list[0])
    key = (n_real, depth)
    if key not in _CACHE:
        _CACHE[key] = build(n_real, depth)
    nc = _CACHE[key]
    hp = host_prep(inp, depth)
    cst, cosT, sinT = host_consts(max(n_real))
    meta = np.asarray(inp["meta_tokens"], np.float32)
    metaT = np.ascontiguousarray(meta.T).reshape(KC, 128, NMETA)
    in_maps = []
    for xs in xs_list:
        m = dict(hp)
        m["cst"] = cst; m["cosT"] = cosT; m["sinT"] = sinT; m["metaT"] = metaT
        for s, x in enumerate(xs):
            m[f"xT{s}"] = np.ascontiguousarray(np.asarray(x, np.float32).T).reshape(KC, 128, x.shape[0])
        in_maps.append(m)
    res = run_bass_kernel_spmd(nc, in_maps, core_ids=list(range(len(xs_list))))
    outs = []
    for r in res.results:
        outs.append([np.ascontiguousarray(r[f"yT{s}"].reshape(D, -1).T) for s in range(len(n_real))])
    return outs


def kernel(x_prompt, x_sample, meta_tokens, **params):
    x_prompt = np.asarray(x_prompt, np.float32); x_sample = np.asarray(x_sample, np.float32)
    inp = dict(params); inp["meta_tokens"] = meta_tokens
    depth = np.asarray(params["w_in"]).shape[0]
    xs_list = [[x_sample[c], x_prompt[c % x_prompt.shape[0]]] for c in range(NCORES)]
    outs = run(xs_list, inp, depth)
    y_sample = np.stack([outs[c][0] for c in range(NCORES)], axis=0)
    y_prompt = np.stack([outs[c][1] for c in range(x_prompt.shape[0])], axis=0)
    return (y_prompt.astype(np.float32), y_sample.astype(np.float32))
```
